# Optimizing a Trainium2 kernel written in Bass

```python
import math
import jax, jax.numpy as jnp
from jax import lax
import numpy as np

D_MODEL = 2048
BATCH = 4
SEQ = 4096
DEPTH = 2

N_MIXERS = 2
N_META = 16
CONV_W = 3
N_HEADS = 16
N_KV_HEADS = 4
HEAD_DIM = D_MODEL // N_HEADS
GROUP = N_HEADS // N_KV_HEADS
KV_DIM = N_KV_HEADS * HEAD_DIM
QKV_DIM = D_MODEL + 2 * KV_DIM
WINDOW = 128
BLOCK = 128
N_BUCKETS = 32
MAX_DISTANCE = 128
D_FF = 5632
EPS = 1e-6
N_A = (DEPTH + 1) // 2
N_B = DEPTH // 2

kernel_name = "hybrid_shortconv_swa_convffn_encoder"


def rmsnorm(x, gain):
    xf = x.astype(jnp.float32)
    y = xf * lax.rsqrt(jnp.mean(xf * xf, axis=-1, keepdims=True) + EPS)
    return (y * gain.astype(jnp.float32)).astype(x.dtype)


def dwconv3(h, w):
    hp = jnp.pad(h, ((0, 0), (1, 1), (0, 0)))
    return hp[:, :-2] * w[0] + hp[:, 1:-1] * w[1] + hp[:, 2:] * w[2]


def t5_bucket(rel):
    half = N_BUCKETS // 2
    max_exact = half // 2
    side = jnp.where(rel > 0, half, 0)
    n = jnp.abs(rel)
    nf = jnp.maximum(n, 1).astype(jnp.float32)
    large = max_exact + (jnp.log(nf / max_exact) / math.log(MAX_DISTANCE / max_exact)
                         * (half - max_exact)).astype(jnp.int32)
    large = jnp.minimum(large, half - 1)
    return side + jnp.where(n < max_exact, n, large)


def short_conv_mixer(x, w_in, conv_w, w_out):
    bch = x @ w_in
    b, c, h = jnp.split(bch, 3, axis=-1)
    return (b * dwconv3(c * h, conv_w)) @ w_out


def windowed_gqa(x, w_qkv, q_gain, k_gain, sink, w_o, rel_table):
    bsz, t, _ = x.shape
    lead = BLOCK - N_META
    tp = t + lead
    nb = tp // BLOCK
    scale = HEAD_DIM ** -0.5

    qkv = x @ w_qkv
    q, k, v = jnp.split(qkv, [D_MODEL, D_MODEL + KV_DIM], axis=-1)
    q = rmsnorm(q.reshape(bsz, t, N_HEADS, HEAD_DIM), q_gain)
    k = rmsnorm(k.reshape(bsz, t, N_KV_HEADS, HEAD_DIM), k_gain)
    v = v.reshape(bsz, t, N_KV_HEADS, HEAD_DIM)

    qb = jnp.pad(q, ((0, 0), (lead, 0), (0, 0), (0, 0))).reshape(
        bsz, nb, BLOCK, N_KV_HEADS, GROUP, HEAD_DIM)
    pad_kv = ((0, 0), (lead + BLOCK, BLOCK), (0, 0), (0, 0))
    kb = jnp.pad(k, pad_kv).reshape(bsz, nb + 2, BLOCK, N_KV_HEADS, HEAD_DIM)
    vb = jnp.pad(v, pad_kv).reshape(bsz, nb + 2, BLOCK, N_KV_HEADS, HEAD_DIM)
    k_band = jnp.concatenate([kb[:, :-2], kb[:, 1:-1], kb[:, 2:]], axis=2)
    v_band = jnp.concatenate([vb[:, :-2], vb[:, 1:-1], vb[:, 2:]], axis=2)
    k_meta = k[:, :N_META]
    v_meta = v[:, :N_META]

    qi = jnp.arange(BLOCK, dtype=jnp.int32)
    ki = jnp.arange(3 * BLOCK, dtype=jnp.int32)
    rel_band = ki[None, :] - BLOCK - qi[:, None]
    blk = jnp.arange(nb, dtype=jnp.int32)
    kpos = (blk[:, None] - 1) * BLOCK + ki[None, :]
    key_ok = (kpos >= BLOCK) & (kpos < tp)
    band_mask = (jnp.abs(rel_band) <= WINDOW)[None] & key_ok[:, None, :]

    band_bias = rel_table[t5_bucket(rel_band)]
    band_bias = jnp.transpose(band_bias, (2, 0, 1)).reshape(
        N_KV_HEADS, GROUP, BLOCK, 3 * BLOCK).astype(jnp.float32)
    qpos = blk[:, None] * BLOCK + qi[None, :]
    mpos = lead + jnp.arange(N_META, dtype=jnp.int32)
    meta_bias = rel_table[t5_bucket(mpos[None, None, :] - qpos[:, :, None])]
    meta_bias = jnp.transpose(meta_bias, (0, 3, 1, 2)).reshape(
        nb, N_KV_HEADS, GROUP, BLOCK, N_META).astype(jnp.float32)

    s_band = jnp.einsum('bnqgrd,bnkgd->bngrqk', qb, k_band).astype(jnp.float32) * scale
    s_band = jnp.where(band_mask[None, :, None, None], s_band + band_bias[None, None], -1e30)
    s_meta = jnp.einsum('bnqgrd,bmgd->bngrqm', qb, k_meta).astype(jnp.float32) * scale
    s_meta = s_meta + meta_bias[None]
    s_sink = jnp.broadcast_to(
        sink.astype(jnp.float32).reshape(1, 1, N_KV_HEADS, GROUP, 1, 1),
        (bsz, nb, N_KV_HEADS, GROUP, BLOCK, 1))
    p = jax.nn.softmax(jnp.concatenate([s_meta, s_band, s_sink], axis=-1), axis=-1)
    p_meta = p[..., :N_META].astype(v.dtype)
    p_band = p[..., N_META:N_META + 3 * BLOCK].astype(v.dtype)

    o = (jnp.einsum('bngrqm,bmgd->bnqgrd', p_meta, v_meta)
         + jnp.einsum('bngrqk,bnkgd->bnqgrd', p_band, v_band))
    o = o.reshape(bsz, tp, D_MODEL)[:, lead:]
    return o @ w_o


def conv_ffn(x, w_up, conv_w, conv_b, w_down):
    g, u = jnp.split(x @ w_up, 2, axis=-1)
    g = dwconv3(g, conv_w) + conv_b
    return (jax.nn.gelu(g, approximate=False) * u) @ w_down


def setup_inputs(seed: int = 0) -> dict:
    key = jax.random.key(seed)
    ks = jax.random.split(key, 18)
    nrm = jax.random.normal
    f32 = jnp.float32
    d = D_MODEL
    return {
        "x": nrm(ks[0], (BATCH, SEQ, d), f32),
        "meta_tokens": nrm(ks[1], (N_META, d), f32),
        "rel_bias_table": 0.1 * nrm(ks[2], (N_BUCKETS, N_HEADS), f32),
        "norm_mix": 1.0 + 0.01 * nrm(ks[3], (DEPTH, d), f32),
        "norm_ffn": 1.0 + 0.01 * nrm(ks[4], (DEPTH, d), f32),
        "conv_in_w": nrm(ks[5], (N_A, d, 3 * d), f32) * d ** -0.5,
        "conv_dw": nrm(ks[6], (N_A, CONV_W, d), f32) * CONV_W ** -0.5,
        "conv_out_w": nrm(ks[7], (N_A, d, d), f32) * d ** -0.5,
        "attn_qkv": nrm(ks[8], (N_B, d, QKV_DIM), f32) * d ** -0.5,
        "attn_q_gain": 1.0 + 0.01 * nrm(ks[9], (N_B, HEAD_DIM), f32),
        "attn_k_gain": 1.0 + 0.01 * nrm(ks[10], (N_B, HEAD_DIM), f32),
        "attn_sink": 0.5 * nrm(ks[11], (N_B, N_HEADS), f32),
        "attn_o": nrm(ks[12], (N_B, d, d), f32) * d ** -0.5,
        "ffn_up": nrm(ks[13], (DEPTH, d, 2 * D_FF), f32) * d ** -0.5,
        "ffn_dw": nrm(ks[14], (DEPTH, CONV_W, D_FF), f32) * CONV_W ** -0.5,
        "ffn_dw_b": 0.01 * nrm(ks[15], (DEPTH, D_FF), f32),
        "ffn_down": nrm(ks[16], (DEPTH, D_FF, d), f32) * D_FF ** -0.5,
    }


def reference(x, meta_tokens, rel_bias_table, norm_mix, norm_ffn,
              conv_in_w, conv_dw, conv_out_w,
              attn_qkv, attn_q_gain, attn_k_gain, attn_sink, attn_o,
              ffn_up, ffn_dw, ffn_dw_b, ffn_down):
    bsz = x.shape[0]
    meta = jnp.broadcast_to(meta_tokens[None].astype(x.dtype), (bsz, N_META, D_MODEL))
    h = jnp.concatenate([meta, x], axis=1)
    for i in range(DEPTH):
        hn = rmsnorm(h, norm_mix[i])
        j = i // N_MIXERS
        if i % N_MIXERS == 0:
            h = h + short_conv_mixer(hn, conv_in_w[j], conv_dw[j], conv_out_w[j])
        else:
            h = h + windowed_gqa(hn, attn_qkv[j], attn_q_gain[j], attn_k_gain[j],
                                 attn_sink[j], attn_o[j], rel_bias_table)
        h = h + conv_ffn(rmsnorm(h, norm_ffn[i]), ffn_up[i], ffn_dw[i], ffn_dw_b[i], ffn_down[i])
    return h[:, N_META:]
```

```python
import contextlib
import numpy as np
import concourse.bass as bass
import concourse.mybir as mybir
from concourse.bass_utils import run_bass_kernel_spmd

F32 = mybir.dt.float32
BF16 = mybir.dt.bfloat16
AF = mybir.ActivationFunctionType
ALU = mybir.AluOpType

D = 2048
KC = 16
DFF = 5632
FC = 44
NMETA = 16
SEQ = 4096
TTOT = SEQ + NMETA
AUX = 24
LM = 2194
L1 = AUX + LM
OFF1 = 1920
NKV = 2192
KVW = 2304
P4_C0 = 14
P4_W = NKV - P4_C0
EPS = 1e-6
NEG = -1.0e30
SCALE = 128.0 ** -0.5

CV_NM0, CV_NF0, CV_NM1, CV_NF1 = 0, 16, 32, 48
CV_CDW = 64
CV_FDW0 = 112
CV_FDW1 = 244
CV_FB0 = 376
CV_FB1 = 420
CV_QG = 464
CV_KG = 465
CV_EPS = 466
CV_ZERO = 467
NCV = 468

ENGS = ("pe", "act", "dve", "pool", "sp")


class Op:
    __slots__ = ("eng", "fn", "deps", "signal", "sigidx", "dma_sem", "dma_val")

    def __init__(self, eng, fn):
        self.eng = eng
        self.fn = fn
        self.deps = []
        self.signal = False
        self.sigidx = None
        self.dma_sem = None
        self.dma_val = None


class Sync:
    def __init__(self, nc, st, dma_keys):
        self.esem = {e: st.enter_context(nc.semaphore("s_" + e)) for e in ENGS}
        self.dsem = {k: st.enter_context(nc.semaphore("d_" + k)) for k in dma_keys}
        self.ecount = {e: 0 for e in ENGS}
        self.dcount = {k: 0 for k in dma_keys}


class Prog:
    def __init__(self, sync):
        self.sync = sync
        self.ops = {e: [] for e in ENGS}
        self.last_w = {}
        self.readers = {}
        self.dcount = dict(sync.dcount)

    def _add(self, eng, fn, reads, writes, dma_sem=None):
        op = Op(eng, fn)
        if dma_sem is not None:
            c = self.dcount[dma_sem] + 1
            self.dcount[dma_sem] = c
            op.dma_sem = dma_sem
            op.dma_val = 16 * c
        deps = []
        for k in reads:
            w = self.last_w.get(k)
            if w is not None:
                deps.append(w)
        for k in writes:
            w = self.last_w.get(k)
            if w is not None:
                deps.append(w)
            deps.extend(self.readers.get(k, ()))
        seen = set()
        for d in deps:
            if d is op or id(d) in seen:
                continue
            seen.add(id(d))
            if d.dma_sem is None and d.eng == eng:
                continue
            op.deps.append(d)
            if d.dma_sem is None:
                d.signal = True
        for k in reads:
            self.readers.setdefault(k, []).append(op)
        for k in writes:
            self.last_w[k] = op
            self.readers[k] = []
        self.ops[eng].append(op)
        return op

    def op(self, eng, fn, reads=(), writes=()):
        return self._add(eng, fn, reads, writes)

    def dma(self, eng, fn, reads=(), writes=(), sem=None):
        return self._add(eng, fn, reads, writes, dma_sem=sem)

    def emit(self, nc, last=False):
        sy = self.sync
        prev_e = dict(sy.ecount)
        prev_d = dict(sy.dcount)
        for e in ENGS:
            c = sy.ecount[e]
            comp = [o for o in self.ops[e] if o.dma_sem is None]
            if comp:
                comp[-1].signal = True
            for o in comp:
                if o.signal:
                    c += 1
                    o.sigidx = c
            sy.ecount[e] = c
        sy.dcount = dict(self.dcount)
        with nc.Block() as block:
            def make(e):
                def body(eng):
                    known = {}
                    for f in ENGS:
                        if f != e and prev_e[f] > 0:
                            eng.wait_ge(sy.esem[f], prev_e[f])
                            known[("e", f)] = prev_e[f]
                    for k, c in prev_d.items():
                        if c > 0:
                            eng.wait_ge(sy.dsem[k], 16 * c)
                            known[("d", k)] = 16 * c
                    for o in self.ops[e]:
                        need = {}
                        for d in o.deps:
                            if d.dma_sem is not None:
                                key, val = ("d", d.dma_sem), d.dma_val
                            else:
                                key, val = ("e", d.eng), d.sigidx
                            if val > need.get(key, 0):
                                need[key] = val
                        for key, val in need.items():
                            if val > known.get(key, 0):
                                known[key] = val
                                s = sy.dsem[key[1]] if key[0] == "d" else sy.esem[key[1]]
                                eng.wait_ge(s, val)
                        ins = o.fn(eng)
                        if o.dma_sem is not None:
                            ins.then_inc(sy.dsem[o.dma_sem], 16)
                        elif o.signal:
                            ins.then_inc(sy.esem[e], 1)
                    if last and e == "sp":
                        for k, c in sy.dcount.items():
                            if c > 0:
                                eng.wait_ge(sy.dsem[k], 16 * c)
                        for f in ENGS:
                            if f != e and sy.ecount[f] > 0:
                                eng.wait_ge(sy.esem[f], sy.ecount[f])
                return body

            block.tensor(make("pe"))
            block.scalar(make("act"))
            block.vector(make("dve"))
            block.gpsimd(make("pool"))
            block.sync(make("sp"))


class Ring:
    def __init__(self, name, tile, nslots):
        self.name = name
        self.tile = tile
        self.n = nslots
        self.i = 0

    def load(self, P, src, ntiles):
        s = self.i % self.n
        self.i += 1
        t = self.tile
        key = ("w", self.name, s)
        P.dma("pool", lambda e: e.dma_start(out=t[:, s, 0:ntiles, :], in_=src), writes=[key], sem="w%s%d" % (self.name, s))
        return s, key


class RR:
    def __init__(self, items):
        self.items = items
        self.i = 0

    def next(self):
        v = self.items[self.i % len(self.items)]
        self.i += 1
        return v


def subtiles(total, wd, halo):
    step = wd - 2 * halo
    offs = []
    o = 0
    while True:
        if o + wd >= total:
            offs.append(total - wd)
            break
        offs.append(o)
        o += step
    res = []
    for i, o in enumerate(offs):
        lo = 0 if i == 0 else o + halo
        hi = total if i == len(offs) - 1 else offs[i + 1] + halo
        res.append((o, lo, hi))
    return res


def build_nc(stop_after=None, debug=False):
    nc = bass.Bass("TRN2", target_bir_lowering=False)

    def din(name, shape, dt=F32):
        return nc.dram_tensor(name, list(shape), dt, kind="ExternalInput").ap()

    xT = din("xT", [128, KC, L1])
    cvec = din("cvec", [128, NCV])
    w_in = din("w_in", [16, 128, 48, 128])
    w_out = din("w_out", [8, 128, 32, 128])
    w_up = [din("w_up%d" % l, [FC, 128, 32, 128]) for l in range(2)]
    w_dn = [din("w_dn%d" % l, [4, 4, 128, 44, 128]) for l in range(2)]
    w_q = din("w_q", [8, 128, 32, 128])
    w_k = din("w_k", [128, 64, 128])
    w_v = din("w_v", [128, 16, 512])
    w_o = din("w_o", [8, 128, 32, 128])
    relT = din("relT", [32, 16])
    sinkv = din("sinkv", [1, 16])
    ohband = din("ohband", [33, 1024])
    ohm = din("ohm", [33, 288 * 65])
    dbg = debug or stop_after is not None
    h1d = nc.dram_tensor("h1d", [128, KC, L1], F32, kind="ExternalOutput" if dbg else "Internal").ap()
    h2d = nc.dram_tensor("h2d", [128, KC, NKV], F32, kind="ExternalOutput" if dbg else "Internal").ap()
    yT = nc.dram_tensor("yT", [128, KC, P4_W], F32, kind="ExternalOutput").ap()

    dma_keys = ["wA0", "wA1", "wA2", "wA3", "wB0", "wB1", "wB2", "hin0", "hin1", "hin2", "hout0", "hout1", "hout2", "cv", "m0", "m1", "m2", "m3",
                "wk", "wv", "hst0", "hst1", "hld0", "hld1"]

    with contextlib.ExitStack() as st:
        sync = Sync(nc, st, dma_keys)

        def sb(name, shape, dt):
            return st.enter_context(nc.sbuf_tensor(name, list(shape), dt))

        cv = sb("cv", [128, NCV], F32)
        ones_n = sb("ones_n", [128, 128], BF16)
        ones_h = sb("ones_h", [128, 128], BF16)
        ones_1 = sb("ones_1", [128, 128], BF16)
        zrow = sb("zrow", [128, 128], F32)
        psb = [st.enter_context(nc.psum_tensor("psb%d" % i, [128, 512], F32)) for i in range(8)]
        with contextlib.ExitStack() as s00:
            P = Prog(sync)
            P.dma("sp", lambda e: e.dma_start(out=cv[:], in_=cvec), writes=["cv"], sem="cv")
            P.op("dve", lambda e: e.memset(ones_n[:], 1.0 / 2048.0), writes=["ones"])
            P.op("dve", lambda e: e.memset(ones_h[:], 1.0 / 128.0), writes=["ones"])
            P.op("dve", lambda e: e.memset(ones_1[:], 1.0), writes=["ones"])
            P.op("dve", lambda e: e.memset(zrow[:], 0.0), writes=["zrow"])
            P.emit(nc)

        def cvc(c):
            return cv[:, c:c + 1]

        def norm_subtile(P, hb, s, wd, gain0, xn, sq, rstd, bank):
            pk = ("ps", bank)
            for k in range(KC):
                r = k % 4
                P.op("act", lambda e, k=k, r=r: e.activation(out=sq[:, r, 0:wd], in_=hb[:, k, s, 0:wd], func=AF.Square),
                     reads=[("h", s, k)], writes=[("sq", r)])
                P.op("pe", lambda e, k=k, r=r: e.matmul(psb[bank][:, 0:wd], lhsT=ones_n[:], rhs=sq[:, r, 0:wd],
                                                        start=(k == 0), stop=(k == KC - 1)),
                     reads=[("sq", r), "ones"], writes=[pk])
            P.op("act", lambda e: e.activation(out=rstd[:, s, 0:wd], in_=psb[bank][:, 0:wd], func=AF.Ln, bias=cvc(CV_EPS)),
                 reads=[pk, "cv"], writes=[("rstd", s)])
            P.op("act", lambda e: e.activation(out=rstd[:, s, 0:wd], in_=rstd[:, s, 0:wd], func=AF.Exp, scale=-0.5),
                 reads=[("rstd", s)], writes=[("rstd", s)])
            for k in range(KC):
                P.op("dve", lambda e, k=k: e.scalar_tensor_tensor(out=xn[:, k, s, 0:wd], in0=hb[:, k, s, 0:wd], scalar=cvc(gain0 + k),
                                                                in1=rstd[:, s, 0:wd], op0=ALU.mult, op1=ALU.mult),
                     reads=[("h", s, k), ("rstd", s), "cv"], writes=[("xn", s, k)])

        def ffn_block(P, layer, hb, S, wd, xn, abuf, t1, ge, sq, rstd, ring):
            gain0 = CV_NF0 if layer == 0 else CV_NF1
            fdw = CV_FDW0 if layer == 0 else CV_FDW1
            fb = CV_FB0 if layer == 0 else CV_FB1
            for s in range(S):
                norm_subtile(P, hb, s, wd, gain0, xn, sq, rstd, 6 + (s % 2))
            gu_banks = RR([(0, 1), (2, 3)])
            dn_banks = RR([4, 5])
            tr = RR([0, 1])
            for gq in range(4):
                wt = ring.tile
                for ff0 in range(0, 11, 2):
                  ffs = [x for x in (ff0, ff0 + 1) if x < 11]
                  lds = {x: ring.load(P, w_up[layer][gq * 11 + x], 32) for x in ffs}
                  for s in range(S):
                   for ff in ffs:
                    f = gq * 11 + ff
                    slot, wkey = lds[ff]
                    if True:
                        bg, bu = gu_banks.next()
                        for part, b in ((0, bg), (1, bu)):
                            for k in range(KC):
                                P.op("pe", lambda e, b=b, k=k, s=s, slot=slot, part=part: e.matmul(
                                    psb[b][:, 0:wd], lhsT=wt[:, slot, part * 16 + k, :], rhs=xn[:, k, s, 0:wd],
                                    start=(k == 0), stop=(k == KC - 1)),
                                    reads=[wkey, ("xn", s, k)], writes=[("ps", b)])
                        ti = tr.next()
                        kg, ku = ("ps", bg), ("ps", bu)
                        P.op("act", lambda e, bg=bg, ti=ti, f=f: e.activation(
                            out=t1[:, ti, 0:wd], in_=psb[bg][:, 0:wd], func=AF.Identity,
                            scale=cvc(fdw + 44 + f), bias=cvc(fb + f)), reads=[kg, "cv"], writes=[("t1", ti)])
                        P.op("dve", lambda e, bg=bg, ti=ti, f=f: e.scalar_tensor_tensor(
                            out=t1[:, ti, 1:wd], in0=psb[bg][:, 0:wd - 1], scalar=cvc(fdw + f), in1=t1[:, ti, 1:wd],
                            op0=ALU.mult, op1=ALU.add), reads=[kg, ("t1", ti), "cv"], writes=[("t1", ti)])
                        P.op("dve", lambda e, bg=bg, ti=ti, f=f: e.scalar_tensor_tensor(
                            out=t1[:, ti, 0:wd - 1], in0=psb[bg][:, 1:wd], scalar=cvc(fdw + 88 + f), in1=t1[:, ti, 0:wd - 1],
                            op0=ALU.mult, op1=ALU.add), reads=[kg, ("t1", ti), "cv"], writes=[("t1", ti)])
                        P.op("act", lambda e, ti=ti: e.activation(out=ge[:, ti, 0:wd], in_=t1[:, ti, 0:wd], func=AF.Gelu),
                             reads=[("t1", ti)], writes=[("ge", ti)])
                        P.op("dve", lambda e, bu=bu, ti=ti, ff=ff, s=s: e.tensor_tensor(
                            out=abuf[:, ff, s, 0:wd], in0=ge[:, ti, 0:wd], in1=psb[bu][:, 0:wd], op=ALU.mult),
                            reads=[("ge", ti), ku], writes=[("a", s, ff)])
                for mq in range(4):
                    slot, wkey = ring.load(P, w_dn[layer][gq, mq], 44)
                    wt = ring.tile
                    for mm in range(4):
                        m = mq * 4 + mm
                        for s in range(S):
                            b = dn_banks.next()
                            for ko in range(11):
                                P.op("pe", lambda e, b=b, ko=ko, s=s, slot=slot, mm=mm: e.matmul(
                                    psb[b][:, 0:wd], lhsT=wt[:, slot, mm * 11 + ko, :], rhs=abuf[:, ko, s, 0:wd],
                                    start=(ko == 0), stop=(ko == 10)),
                                    reads=[wkey, ("a", s, ko)], writes=[("ps", b)])
                            P.op("dve", lambda e, b=b, m=m, s=s: e.tensor_tensor(
                                out=hb[:, m, s, 0:wd], in0=hb[:, m, s, 0:wd], in1=psb[b][:, 0:wd], op=ALU.add),
                                reads=[("ps", b), ("h", s, m)], writes=[("h", s, m)])

        def mixer_block(P, hb, S, wd, xn, ybuf, csb, chb, t1, sq, rstd, ring):
            for s in range(S):
                norm_subtile(P, hb, s, wd, CV_NM0, xn, sq, rstd, 6 + (s % 2))
            bch_banks = RR([(0, 1, 2), (3, 4, 5)])
            op_banks = RR([0, 1, 2, 3, 4, 5])
            tr = RR([0, 1])
            wt = ring.tile
            for f0 in range(0, 16, 2):
              lds = {x: ring.load(P, w_in[x], 48) for x in (f0, f0 + 1)}
              for s in range(S):
               for f in (f0, f0 + 1):
                slot, wkey = lds[f]
                if True:
                    bb, bc, bh = bch_banks.next()
                    for part, b in ((0, bb), (1, bc), (2, bh)):
                        for k in range(KC):
                            P.op("pe", lambda e, b=b, k=k, s=s, slot=slot, part=part: e.matmul(
                                psb[b][:, 0:wd], lhsT=wt[:, slot, part * 16 + k, :], rhs=xn[:, k, s, 0:wd],
                                start=(k == 0), stop=(k == KC - 1)),
                                reads=[wkey, ("xn", s, k)], writes=[("ps", b)])
                    ti = tr.next()
                    P.op("act", lambda e, bc=bc, ti=ti: e.activation(out=csb[:, ti, 0:wd], in_=psb[bc][:, 0:wd], func=AF.Copy),
                         reads=[("ps", bc)], writes=[("csb", ti)])
                    P.op("dve", lambda e, bh=bh, ti=ti: e.tensor_tensor(out=chb[:, ti, 0:wd], in0=csb[:, ti, 0:wd],
                                                                         in1=psb[bh][:, 0:wd], op=ALU.mult),
                         reads=[("csb", ti), ("ps", bh)], writes=[("chb", ti)])
                    P.op("act", lambda e, ti=ti, f=f: e.activation(out=t1[:, ti, 0:wd], in_=chb[:, ti, 0:wd], func=AF.Identity,
                                                                  scale=cvc(CV_CDW + 16 + f), bias=cvc(CV_ZERO)),
                         reads=[("chb", ti), "cv"], writes=[("t1", ti)])
                    P.op("dve", lambda e, ti=ti, f=f: e.scalar_tensor_tensor(
                        out=t1[:, ti, 1:wd], in0=chb[:, ti, 0:wd - 1], scalar=cvc(CV_CDW + f), in1=t1[:, ti, 1:wd],
                        op0=ALU.mult, op1=ALU.add), reads=[("chb", ti), ("t1", ti), "cv"], writes=[("t1", ti)])
                    P.op("dve", lambda e, ti=ti, f=f: e.scalar_tensor_tensor(
                        out=t1[:, ti, 0:wd - 1], in0=chb[:, ti, 1:wd], scalar=cvc(CV_CDW + 32 + f), in1=t1[:, ti, 0:wd - 1],
                        op0=ALU.mult, op1=ALU.add), reads=[("chb", ti), ("t1", ti), "cv"], writes=[("t1", ti)])
                    P.op("dve", lambda e, bb=bb, ti=ti, f=f, s=s: e.tensor_tensor(
                        out=ybuf[:, f, s, 0:wd], in0=t1[:, ti, 0:wd], in1=psb[bb][:, 0:wd], op=ALU.mult),
                        reads=[("t1", ti), ("ps", bb)], writes=[("a", s, f)])
            for mq in range(8):
                slot, wkey = ring.load(P, w_out[mq], 32)
                wt = ring.tile
                for mm in range(2):
                    m = mq * 2 + mm
                    for s in range(S):
                        b = op_banks.next()
                        for k in range(KC):
                            P.op("pe", lambda e, b=b, k=k, s=s, slot=slot, mm=mm: e.matmul(
                                psb[b][:, 0:wd], lhsT=wt[:, slot, mm * 16 + k, :], rhs=ybuf[:, k, s, 0:wd],
                                start=(k == 0), stop=(k == KC - 1)),
                                reads=[wkey, ("a", s, k)], writes=[("ps", b)])
                        P.op("dve", lambda e, b=b, m=m, s=s: e.tensor_tensor(
                            out=hb[:, m, s, 0:wd], in0=hb[:, m, s, 0:wd], in1=psb[b][:, 0:wd], op=ALU.add),
                            reads=[("ps", b), ("h", s, m)], writes=[("h", s, m)])

        def dense_phase(layer, with_mixer, src, dst, total, src_c0, wd, halo, S):
            with contextlib.ExitStack() as sp:
                def sbp(name, shape, dt):
                    return sp.enter_context(nc.sbuf_tensor("%s_L%d" % (name, layer), list(shape), dt))
                P = Prog(sync)
                hb = sbp("hb", [128, KC, S, wd], F32)
                xn = sbp("xn", [128, KC, S, wd], BF16)
                abuf = sbp("abuf", [128, 16 if with_mixer else 11, S, wd], BF16)
                t1 = sbp("t1", [128, 2, wd], F32)
                ge = sbp("ge", [128, 2, wd], F32)
                csb = sbp("csb", [128, 2, wd], F32) if with_mixer else None
                chb = sbp("chb", [128, 2, wd], F32) if with_mixer else None
                sq = sbp("sq", [128, 4, wd], BF16)
                rstd = sbp("rstd", [128, S, wd], F32)
                wtile = sbp("wring", [128, 4, 48, 128], BF16)
                ring = Ring("A", wtile, 4)
                sts = subtiles(total, wd, halo)
                groups = [sts[i:i + S] for i in range(0, len(sts), S)]
                for grp in groups:
                    ns = len(grp)
                    for s, (o, lo, hi) in enumerate(grp):
                        P.dma("sp", lambda e, s=s, o=o: e.dma_start(out=hb[:, :, s, :], in_=src[:, :, src_c0 + o: src_c0 + o + wd]),
                              writes=[("h", s, k) for k in range(KC)], sem="hin%d" % s)
                    if with_mixer:
                        mixer_block(P, hb, ns, wd, xn, abuf, csb, chb, t1, sq, rstd, ring)
                    ffn_block(P, layer, hb, ns, wd, xn, abuf, t1, ge, sq, rstd, ring)
                    for s, (o, lo, hi) in enumerate(grp):
                        P.dma("sp", lambda e, s=s, o=o, lo=lo, hi=hi: e.dma_start(out=dst[:, :, lo:hi], in_=hb[:, :, s, lo - o:hi - o]),
                              reads=[("h", s, k) for k in range(KC)], sem="hout%d" % s)
                P.emit(nc, last=(layer == 1))

        dense_phase(0, True, xT, h1d, L1, 0, 447, 2, 2)
        if stop_after == 1:
            with contextlib.ExitStack() as sp:
                P = Prog(sync)
                P.emit(nc, last=True)
            return nc

        att = contextlib.ExitStack()

        def sba(name, shape, dt):
            return att.enter_context(nc.sbuf_tensor(name, list(shape), dt))
        KT = sba("KT", [128, 4, KVW], BF16)
        V = sba("V", [128, 18, 512], BF16)
        bband = sba("bband", [128, 3, 16, 128], BF16)
        bm0 = sba("bm0", [128, 16, 128], BF16)
        bm1 = sba("bm1", [128, 16, 128], BF16)
        bmg = sba("bmg", [128, 16, 128], BF16)
        pTM = sba("pTM", [128, 4, 512], BF16)
        with contextlib.ExitStack() as s0:
            P = Prog(sync)
            text = s0.enter_context(nc.sbuf_tensor("text", [33, 16], F32))
            ohb = s0.enter_context(nc.sbuf_tensor("ohb", [33, 1024], F32))
            ohms = s0.enter_context(nc.sbuf_tensor("ohms", [33, 288 * 65], F32))
            sk = s0.enter_context(nc.sbuf_tensor("sk", [65, 16], F32))
            P.op("dve", lambda e: e.memset(text[:], NEG), writes=["text"])
            P.dma("sp", lambda e: e.dma_start(out=text[0:32, :], in_=relT), writes=["text"], sem="m0")
            P.dma("sp", lambda e: e.dma_start(out=ohb[:], in_=ohband), writes=["ohb"], sem="m1")
            P.dma("sp", lambda e: e.dma_start(out=ohms[:], in_=ohm), writes=["ohms"], sem="m2")
            P.dma("sp", lambda e: e.dma_start(out=sk[64:65, :], in_=sinkv), writes=["sk"], sem="m3")
            P.op("dve", lambda e: e.memset(bm0[:], 0.0), writes=["bias"])
            P.op("dve", lambda e: e.memset(pTM[:], 0.0), writes=["pTM"])
            P.op("dve", lambda e: e.memset(KT[:], 0.0), writes=["KT"])
            P.op("dve", lambda e: e.memset(V[:], 0.0), writes=["V"])
            for h in range(16):
                g, hh = divmod(h, 4)
                P.op("act", lambda e, g=g, hh=hh, h=h: e.activation(
                    out=pTM[64:65, g, hh * 128:(hh + 1) * 128], in_=zrow[64:65, :], func=AF.Exp,
                    bias=sk[64:65, h:h + 1], scale=0.0), reads=["zrow", "sk", "pTM"], writes=["pTM"])
            banks = RR([0, 1, 2, 3])

            def bias_round(dst_fn, nrow, oh_fn, q0, nq):
                b = banks.next()
                pk = ("ps", b)
                for qi in range(nq):
                    P.op("pe", lambda e, b=b, qi=qi, q=q0 + qi: e.matmul(
                        psb[b][0:nrow, qi * 16:(qi + 1) * 16], lhsT=oh_fn(q), rhs=text[:, :], start=True, stop=True),
                        reads=["text", "ohb", "ohms"], writes=[pk])
                src = psb[b][0:nrow, 0:nq * 16].rearrange("p (q h) -> p h q", h=16)
                P.op("dve", lambda e, src=src: e.tensor_copy(out=dst_fn(q0, nq), in_=src), reads=[pk], writes=["bias"])

            for kbt in range(3):
                for q0 in range(0, 128, 32):
                    bias_round(lambda q0_, nq_, kbt=kbt: bband[:, kbt, :, q0_:q0_ + nq_], 128,
                               lambda q, kbt=kbt: ohb[:, 128 * (kbt - 1) - q + 511: 128 * (kbt - 1) - q + 511 + 128], q0, 32)
            bias_round(lambda q0_, nq_: bm0[0:65, :, q0_:q0_ + nq_], 65,
                       lambda q: ohms[:, (q - 96) * 65:(q - 95) * 65], 96, 32)
            for q0 in range(0, 128, 32):
                bias_round(lambda q0_, nq_: bm1[0:65, :, q0_:q0_ + nq_], 65,
                           lambda q: ohms[:, (32 + q) * 65:(33 + q) * 65], q0, 32)
            for q0 in range(0, 128, 32):
                bias_round(lambda q0_, nq_: bmg[0:65, :, q0_:q0_ + nq_], 65,
                           lambda q: ohms[:, (160 + q) * 65:(161 + q) * 65], q0, 32)
            P.emit(nc)

        with contextlib.ExitStack() as sp:
            def sbp(name, shape, dt):
                return sp.enter_context(nc.sbuf_tensor(name, list(shape), dt))
            P = Prog(sync)
            hb = sbp("hb2", [128, KC, 1, 512], F32)
            xn = sbp("xn2", [128, KC, 1, 512], BF16)
            sq = sbp("sq2", [128, 4, 512], BF16)
            rstd = sbp("rstd2", [128, 1, 512], F32)
            rk = sbp("rk2", [128, 2, 512], F32)
            wk = sbp("wk", [128, 64, 128], BF16)
            wv = sbp("wv", [128, 16, 512], BF16)
            P.dma("pool", lambda e: e.dma_start(out=wk[:], in_=w_k), writes=["wk"], sem="wk")
            P.dma("pool", lambda e: e.dma_start(out=wv[:], in_=w_v), writes=["wv"], sem="wv")
            kb = RR([0, 1])
            sb_ = RR([2, 3])
            vb = RR([4, 5])
            rr = RR([0, 1])
            sqr = RR([0, 1, 2, 3])
            hkeys = [("h", 0, k) for k in range(KC)]
            for t in range(5):
                nb = 4 if t < 4 else 2
                w = nb * 128
                kv0 = t * 512
                if t == 0:
                    P.op("dve", lambda e: e.memset(hb[:, :, 0, 0:128], 0.0), writes=hkeys)
                    P.dma("sp", lambda e: e.dma_start(out=hb[:, :, 0, 0:16], in_=h1d[:, :, AUX:AUX + 16]), writes=hkeys, sem="hin0")
                    P.dma("sp", lambda e: e.dma_start(out=hb[:, :, 0, 32:48], in_=h1d[:, :, 0:16]), writes=hkeys, sem="hin1")
                    P.dma("sp", lambda e: e.dma_start(out=hb[:, :, 0, 128:512], in_=h1d[:, :, AUX + 16:AUX + 400]), writes=hkeys, sem="hin2")
                else:
                    c0 = AUX + 16 + (4 * t - 1) * 128
                    P.dma("sp", lambda e, c0=c0, w=w: e.dma_start(out=hb[:, :, 0, 0:w], in_=h1d[:, :, c0:c0 + w]), writes=hkeys, sem="hin0")
                norm_subtile(P, hb, 0, w, CV_NM1, xn, sq, rstd, 6 + (t % 2))
                for g in range(4):
                    b = kb.next()
                    for k in range(KC):
                        P.op("pe", lambda e, b=b, k=k, g=g, w=w: e.matmul(psb[b][:, 0:w], lhsT=wk[:, g * 16 + k, :], rhs=xn[:, k, 0, 0:w],
                                                                      start=(k == 0), stop=(k == KC - 1)),
                             reads=["wk", ("xn", 0, k)], writes=[("ps", b)])
                    r = sqr.next()
                    P.op("act", lambda e, b=b, r=r, w=w: e.activation(out=sq[:, r, 0:w], in_=psb[b][:, 0:w], func=AF.Square),
                         reads=[("ps", b)], writes=[("sq", r)])
                    b2 = sb_.next()
                    P.op("pe", lambda e, b2=b2, r=r, w=w: e.matmul(psb[b2][:, 0:w], lhsT=ones_h[:], rhs=sq[:, r, 0:w], start=True, stop=True),
                         reads=[("sq", r), "ones"], writes=[("ps", b2)])
                    ri = rr.next()
                    P.op("act", lambda e, b2=b2, ri=ri, w=w: e.activation(out=rk[:, ri, 0:w], in_=psb[b2][:, 0:w], func=AF.Ln, bias=cvc(CV_EPS)),
                         reads=[("ps", b2), "cv"], writes=[("rk", ri)])
                    P.op("act", lambda e, ri=ri, w=w: e.activation(out=rk[:, ri, 0:w], in_=rk[:, ri, 0:w], func=AF.Exp, scale=-0.5),
                         reads=[("rk", ri)], writes=[("rk", ri)])
                    P.op("dve", lambda e, b=b, ri=ri, g=g, w=w, kv0=kv0: e.scalar_tensor_tensor(
                        out=KT[:, g, kv0:kv0 + w], in0=psb[b][:, 0:w], scalar=cvc(CV_KG), in1=rk[:, ri, 0:w], op0=ALU.mult, op1=ALU.mult),
                        reads=[("ps", b), ("rk", ri), "cv"], writes=["KT"])
                for j in range(nb):
                    b = vb.next()
                    for k in range(KC):
                        P.op("pe", lambda e, b=b, k=k, j=j: e.matmul(psb[b][:, 0:512], lhsT=xn[:, k, 0, j * 128:(j + 1) * 128], rhs=wv[:, k, :],
                                                                 start=(k == 0), stop=(k == KC - 1)),
                             reads=["wv", ("xn", 0, k)], writes=[("ps", b)])
                    P.op("act", lambda e, b=b, bi=4 * t + j: e.activation(out=V[:, bi, :], in_=psb[b][:, 0:512], func=AF.Copy),
                         reads=[("ps", b)], writes=["V"])
            P.emit(nc)

        with contextlib.ExitStack() as sp:
            def sbp(name, shape, dt):
                return sp.enter_context(nc.sbuf_tensor(name, list(shape), dt))
            P = Prog(sync)
            WQ = 384
            hb = sbp("hb3", [128, KC, 1, WQ], F32)
            xn = sbp("xn3", [128, KC, 1, WQ], BF16)
            qn = sbp("qn3", [128, 16, WQ], BF16)
            ob = sbp("ob3", [128, 16, WQ], BF16)
            sq = sbp("sq3", [128, 4, WQ], BF16)
            rstd = sbp("rstd3", [128, 1, WQ], F32)
            rq = sbp("rq3", [128, 2, WQ], F32)
            ssb = sbp("ssb3", [128, 2, 512], F32)
            pT = sbp("pT3", [128, 6, 512], BF16)
            rden = sbp("rden3", [128, 2, 512], F32)
            hst = sbp("hst3", [128, 2, WQ], F32)
            wtile = sbp("wring3", [128, 3, 32, 128], BF16)
            ring = Ring("B", wtile, 3)
            hkeys = [("h", 0, k) for k in range(KC)]
            tiles = [(16 + (3 * i - 1) * 128, 384, [3 * i, 3 * i + 1, 3 * i + 2]) for i in range(0, 6)]
            qb = RR([0, 1])
            s2b = RR([2, 3])
            rr = RR([0, 1])
            sqr = RR([0, 1, 2, 3])
            ssr = RR([0, 1])
            ptr = RR([0, 1, 2, 3, 4, 5])
            ob_ = RR([4, 5])
            db_ = RR([6, 7])
            rdr = RR([0, 1])
            hsr = RR([0, 1])
            opb = RR([0, 1, 2, 3])
            for (c0, w, blocks) in tiles:
                if c0 < 0:
                    P.op("dve", lambda e: e.memset(hb[:, :, 0, 0:112], 0.0), writes=hkeys)
                    P.dma("sp", lambda e: e.dma_start(out=hb[:, :, 0, 112:384], in_=h1d[:, :, AUX:AUX + 272]), writes=hkeys, sem="hin0")
                else:
                    P.dma("sp", lambda e, c0=c0, w=w: e.dma_start(out=hb[:, :, 0, 0:w], in_=h1d[:, :, AUX + c0:AUX + c0 + w]),
                          writes=hkeys, sem="hin0")
                norm_subtile(P, hb, 0, w, CV_NM1, xn, sq, rstd, 6)
                for hq2 in range(8):
                    slot, wkey = ring.load(P, w_q[hq2], 32)
                    for hh in range(2):
                        hq = hq2 * 2 + hh
                        b = qb.next()
                        for k in range(KC):
                            P.op("pe", lambda e, b=b, k=k, slot=slot, hh=hh, w=w: e.matmul(
                                psb[b][:, 0:w], lhsT=wtile[:, slot, hh * 16 + k, :], rhs=xn[:, k, 0, 0:w],
                                start=(k == 0), stop=(k == KC - 1)), reads=[wkey, ("xn", 0, k)], writes=[("ps", b)])
                        r = sqr.next()
                        P.op("act", lambda e, b=b, r=r, w=w: e.activation(out=sq[:, r, 0:w], in_=psb[b][:, 0:w], func=AF.Square),
                             reads=[("ps", b)], writes=[("sq", r)])
                        b2 = s2b.next()
                        P.op("pe", lambda e, b2=b2, r=r, w=w: e.matmul(psb[b2][:, 0:w], lhsT=ones_h[:], rhs=sq[:, r, 0:w], start=True, stop=True),
                             reads=[("sq", r), "ones"], writes=[("ps", b2)])
                        ri = rr.next()
                        P.op("act", lambda e, b2=b2, ri=ri, w=w: e.activation(out=rq[:, ri, 0:w], in_=psb[b2][:, 0:w], func=AF.Ln, bias=cvc(CV_EPS)),
                             reads=[("ps", b2), "cv"], writes=[("rq", ri)])
                        P.op("act", lambda e, ri=ri, w=w: e.activation(out=rq[:, ri, 0:w], in_=rq[:, ri, 0:w], func=AF.Exp, scale=-0.5),
                             reads=[("rq", ri)], writes=[("rq", ri)])
                        P.op("dve", lambda e, b=b, ri=ri, hq=hq, w=w: e.scalar_tensor_tensor(
                            out=qn[:, hq, 0:w], in0=psb[b][:, 0:w], scalar=cvc(CV_QG), in1=rq[:, ri, 0:w], op0=ALU.mult, op1=ALU.mult),
                            reads=[("ps", b), ("rq", ri), "cv"], writes=[("qn", hq)])
                units = []
                for jb in blocks:
                    ql, nq = 16 + (jb - 1) * 128 - c0, 128
                    kbl = []
                    if jb == 0:
                        kbl.append(("M", 0, 65, lambda g: bm0[0:65, 4 * g:4 * g + 4, :]))
                        kbl.append(("B", 128, 128, lambda g: bband[:, 2, 4 * g:4 * g + 4, :]))
                    else:
                        kbl.append(("M", 0, 65, (lambda g: bm1[0:65, 4 * g:4 * g + 4, :]) if jb == 1 else (lambda g: bmg[0:65, 4 * g:4 * g + 4, :])))
                        if jb >= 2:
                            kbl.append(("B", (jb - 1) * 128, 128, lambda g: bband[:, 0, 4 * g:4 * g + 4, :]))
                        kbl.append(("B", jb * 128, 128, lambda g: bband[:, 1, 4 * g:4 * g + 4, :]))
                        if jb <= 16:
                            kbl.append(("B", (jb + 1) * 128, 128, lambda g: bband[:, 2, 4 * g:4 * g + 4, :]))
                    for g in range(4):
                        units.append({"g": g, "ql": ql, "kbl": kbl, "pts": {}, "bo": None, "bd": None})
                n4 = 512

                def rec_S(u, k):
                    kind, kc0, nr, bfn = u["kbl"][k]
                    g, ql = u["g"], u["ql"]
                    b = k
                    P.op("pe", lambda e, b=b, kc0=kc0, nr=nr, g=g, ql=ql: e.matmul(
                        psb[b][0:nr, 0:n4], lhsT=KT[:, g, kc0:kc0 + nr], rhs=qn[:, 4 * g:4 * g + 4, ql:ql + 128],
                        start=True, stop=True), reads=["KT"] + [("qn", 4 * g + i) for i in range(4)], writes=[("ps", b)])
                    si = ssr.next()
                    P.op("dve", lambda e, b=b, nr=nr, si=si, bfn=bfn, g=g: e.scalar_tensor_tensor(
                        out=ssb[0:nr, si, 0:n4].rearrange("p (h q) -> p h q", h=4),
                        in0=psb[b][0:nr, 0:n4].rearrange("p (h q) -> p h q", h=4), scalar=SCALE,
                        in1=bfn(g), op0=ALU.mult, op1=ALU.add), reads=[("ps", b), "bias"], writes=[("ssb", si)])
                    if kind == "M":
                        P.op("act", lambda e, si=si, g=g: e.activation(out=pTM[0:64, g, 0:n4], in_=ssb[0:64, si, 0:n4], func=AF.Exp),
                             reads=[("ssb", si)], writes=[("pTM", g)])
                        u["pts"][k] = ("M", kc0, nr, None)
                    else:
                        pi = ptr.next()
                        P.op("act", lambda e, si=si, pi=pi: e.activation(out=pT[:, pi, 0:n4], in_=ssb[:, si, 0:n4], func=AF.Exp),
                             reads=[("ssb", si)], writes=[("pT", pi)])
                        u["pts"][k] = ("B", kc0, nr, pi)

                def rec_PV(u, k):
                    if u["bo"] is None:
                        u["bo"] = ob_.next()
                        u["bd"] = db_.next()
                    bo, bd, g = u["bo"], u["bd"], u["g"]
                    kind, kc0, nr, pi = u["pts"][k]
                    first, lastk = (k == 0), (k == len(u["kbl"]) - 1)
                    vblk = kc0 // 128
                    if kind == "M":
                        rhs_fn = lambda g=g: pTM[0:65, g, 0:n4]
                        rk_ = [("pTM", g), "pTM"]
                    else:
                        rhs_fn = lambda pi=pi: pT[:, pi, 0:n4]
                        rk_ = [("pT", pi)]
                    P.op("pe", lambda e, bo=bo, vblk=vblk, nr=nr, g=g, rhs_fn=rhs_fn, first=first, lastk=lastk: e.matmul(
                        psb[bo][:, 0:n4], lhsT=V[0:nr, vblk, g * 128:(g + 1) * 128], rhs=rhs_fn(), start=first, stop=lastk),
                        reads=["V"] + rk_, writes=[("ps", bo)])
                    P.op("pe", lambda e, bd=bd, nr=nr, rhs_fn=rhs_fn, first=first, lastk=lastk: e.matmul(
                        psb[bd][:, 0:n4], lhsT=ones_1[0:nr, :], rhs=rhs_fn(), start=first, stop=lastk),
                        reads=["ones"] + rk_, writes=[("ps", bd)])

                def rec_fin(u):
                    bo, bd, g, ql = u["bo"], u["bd"], u["g"], u["ql"]
                    rdi = rdr.next()
                    P.op("act", lambda e, bd=bd, rdi=rdi: e.activation(out=rden[:, rdi, 0:n4], in_=psb[bd][:, 0:n4], func=AF.Ln),
                         reads=[("ps", bd)], writes=[("rden", rdi)])
                    P.op("act", lambda e, rdi=rdi: e.activation(out=rden[:, rdi, 0:n4], in_=rden[:, rdi, 0:n4], func=AF.Exp, scale=-1.0),
                         reads=[("rden", rdi)], writes=[("rden", rdi)])
                    P.op("dve", lambda e, bo=bo, rdi=rdi, g=g, ql=ql: e.tensor_tensor(
                        out=ob[:, 4 * g:4 * g + 4, ql:ql + 128], in0=psb[bo][:, 0:n4].rearrange("p (h q) -> p h q", h=4),
                        in1=rden[:, rdi, 0:n4].rearrange("p (h q) -> p h q", h=4), op=ALU.mult),
                        reads=[("ps", bo), ("rden", rdi)], writes=[("ob", 4 * g + i) for i in range(4)])

                for k in range(len(units[0]["kbl"])):
                    rec_S(units[0], k)
                for i, u in enumerate(units):
                    nxt = units[i + 1] if i + 1 < len(units) else None
                    nk = len(u["kbl"])
                    nn = len(nxt["kbl"]) if nxt is not None else 0
                    for k in range(max(nk, nn)):
                        if k < nk:
                            rec_PV(u, k)
                        if k < nn:
                            rec_S(nxt, k)
                    rec_fin(u)
                for mq in range(8):
                    slot, wkey = ring.load(P, w_o[mq], 32)
                    for mm in range(2):
                        m = mq * 2 + mm
                        b = opb.next()
                        for k in range(KC):
                            P.op("pe", lambda e, b=b, k=k, slot=slot, mm=mm, w=w: e.matmul(
                                psb[b][:, 0:w], lhsT=wtile[:, slot, mm * 16 + k, :], rhs=ob[:, k, 0:w],
                                start=(k == 0), stop=(k == KC - 1)), reads=[wkey, ("ob", k)], writes=[("ps", b)])
                        hi = hsr.next()
                        P.op("dve", lambda e, b=b, m=m, hi=hi, w=w: e.tensor_tensor(out=hst[:, hi, 0:w], in0=hb[:, m, 0, 0:w], in1=psb[b][:, 0:w], op=ALU.add),
                             reads=[("ps", b), ("h", 0, m)], writes=[("hst", hi)])
                        sc = 112 if c0 < 0 else 0
                        P.dma("sp", lambda e, m=m, hi=hi, w=w, c0=c0, sc=sc: e.dma_start(out=h2d[:, m, c0 + sc:c0 + w], in_=hst[:, hi, sc:w]),
                              reads=[("hst", hi)], sem="hst%d" % hi)
            P.emit(nc)
        if stop_after == 3:
            with contextlib.ExitStack() as sp:
                P = Prog(sync)
                P.emit(nc, last=True)
            att.close()
            return nc

        att.close()
        dense_phase(1, False, h2d, yT, P4_W, P4_C0, 438, 1, 2)
    return nc


def _t5_bucket(rel):
    rel = np.asarray(rel, dtype=np.int64)
    side = np.where(rel > 0, 16, 0)
    n = np.abs(rel)
    nf = np.maximum(n, 1).astype(np.float32)
    large = 8 + (np.log(nf / np.float32(8)) / np.float32(np.log(16.0)) * np.float32(8)).astype(np.int32)
    large = np.minimum(large, 15)
    return side + np.where(n < 8, n, large)


def _onehot_cols(bk):
    oh = np.zeros((33,) + bk.shape, np.float32)
    idx = np.indices(bk.shape)
    oh[(bk,) + tuple(idx)] = 1.0
    return oh


def _oh_band():
    rel = np.arange(1024) - 511
    bk = _t5_bucket(rel)
    bk = np.where(np.abs(rel) <= 128, bk, 32)
    return _onehot_cols(bk)


def _oh_m(half):
    bk = np.full((288, 65), 32, np.int64)
    r = np.arange(16)
    for slot in range(288):
        if slot < 32:
            var, qidx, j = 0, 96 + slot, 0
        elif slot < 160:
            var, qidx, j = 1, slot - 32, 1
        else:
            var, qidx, j = 2, slot - 160, 2
        rel = (112 + r) - (128 * j + qidx)
        if half == 0:
            bk[slot, 0:16] = _t5_bucket(rel)
        else:
            if var == 1:
                bk[slot, 0:16] = np.where(np.abs(rel) <= 128, _t5_bucket(rel), 32)
            bk[slot, 32:48] = 15
    return _onehot_cols(bk).reshape(33, 288 * 65)


def _chunkT(a):
    c = a.shape[0]
    return np.ascontiguousarray(a.reshape(c, 16, 128).transpose(2, 1, 0))


def _wtiles(w, n0, n1):
    K = w.shape[0]
    return w[:, n0:n1].reshape(K // 128, 128, n1 - n0).transpose(1, 0, 2)


def _prep_weights(inp):
    f = np.float32
    cin = inp["conv_in_w"][0]
    w_in = np.empty((16, 128, 48, 128), f)
    for fc in range(16):
        for p in range(3):
            w_in[fc, :, p * 16:(p + 1) * 16, :] = _wtiles(cin, p * D + fc * 128, p * D + fc * 128 + 128)

    def pairs(w):
        o = np.empty((8, 128, 32, 128), f)
        for m in range(16):
            o[m // 2, :, (m % 2) * 16:(m % 2) * 16 + 16, :] = _wtiles(w, m * 128, m * 128 + 128)
        return o
    w_out = pairs(inp["conv_out_w"][0])
    qkv = inp["attn_qkv"][0]
    w_q = pairs(qkv[:, 0:D])
    w_o = pairs(inp["attn_o"][0])
    w_k = np.empty((128, 64, 128), f)
    for g in range(4):
        w_k[:, g * 16:(g + 1) * 16, :] = _wtiles(qkv, D + g * 128, D + g * 128 + 128)
    w_v = np.ascontiguousarray(_wtiles(qkv, D + 512, D + 1024))
    w_up, w_dn = [], []
    for l in range(2):
        up = inp["ffn_up"][l]
        u = np.empty((FC, 128, 32, 128), f)
        for fc in range(FC):
            u[fc, :, 0:16, :] = _wtiles(up, fc * 128, fc * 128 + 128)
            u[fc, :, 16:32, :] = _wtiles(up, DFF + fc * 128, DFF + fc * 128 + 128)
        w_up.append(u)
        dn = inp["ffn_down"][l]
        d = np.empty((4, 4, 128, 44, 128), f)
        for gq in range(4):
            rows = dn[gq * 11 * 128:(gq + 1) * 11 * 128]
            for mq in range(4):
                for mm in range(4):
                    m = mq * 4 + mm
                    d[gq, mq, :, mm * 11:(mm + 1) * 11, :] = _wtiles(rows, m * 128, m * 128 + 128)
        w_dn.append(d)
    cv = np.zeros((128, NCV), f)

    def col(v):
        return v.reshape(-1, 128).T
    cv[:, CV_NM0:CV_NM0 + 16] = col(inp["norm_mix"][0])
    cv[:, CV_NF0:CV_NF0 + 16] = col(inp["norm_ffn"][0])
    cv[:, CV_NM1:CV_NM1 + 16] = col(inp["norm_mix"][1])
    cv[:, CV_NF1:CV_NF1 + 16] = col(inp["norm_ffn"][1])
    for t in range(3):
        cv[:, CV_CDW + 16 * t:CV_CDW + 16 * t + 16] = col(inp["conv_dw"][0, t])
        cv[:, CV_FDW0 + 44 * t:CV_FDW0 + 44 * t + 44] = col(inp["ffn_dw"][0, t])
        cv[:, CV_FDW1 + 44 * t:CV_FDW1 + 44 * t + 44] = col(inp["ffn_dw"][1, t])
    cv[:, CV_FB0:CV_FB0 + 44] = col(inp["ffn_dw_b"][0])
    cv[:, CV_FB1:CV_FB1 + 44] = col(inp["ffn_dw_b"][1])
    cv[:, CV_QG] = inp["attn_q_gain"][0]
    cv[:, CV_KG] = inp["attn_k_gain"][0]
    cv[:, CV_EPS] = EPS
    shared = {"cvec": cv, "w_in": w_in, "w_out": w_out, "w_up0": w_up[0], "w_up1": w_up[1],
              "w_dn0": w_dn[0], "w_dn1": w_dn[1], "w_q": w_q, "w_k": w_k, "w_v": w_v, "w_o": w_o,
              "relT": np.ascontiguousarray(inp["rel_bias_table"], dtype=f),
              "sinkv": np.ascontiguousarray(inp["attn_sink"].reshape(1, 16), dtype=f),
              "ohband": _oh_band()}
    return shared


def _core_x(inp, core):
    b, half = divmod(core, 2)
    x = inp["x"][b]
    meta = inp["meta_tokens"]
    full = np.concatenate([meta, x], axis=0)
    cols = np.zeros((L1, D), np.float32)
    cols[0:18] = full[0:18]
    p0 = 0 if half == 0 else OFF1
    n = min(LM, TTOT - p0)
    cols[AUX:AUX + n] = full[p0:p0 + n]
    return _chunkT(cols)


def make_in_maps(inp):
    inp = {k: np.asarray(v) for k, v in inp.items()}
    shared = _prep_weights(inp)
    ohms = [_oh_m(0), _oh_m(1)]
    maps = []
    for core in range(8):
        m = dict(shared)
        m["xT"] = _core_x(inp, core)
        m["ohm"] = ohms[core % 2]
        maps.append(m)
    return maps


def assemble(results):
    out = np.empty((4, SEQ, D), np.float32)
    for core in range(8):
        b, half = divmod(core, 2)
        y = results[core]["yT"]
        yt = y.transpose(2, 1, 0).reshape(P4_W, D)
        if half == 0:
            out[b, 0:2047] = yt[2:2049]
        else:
            out[b, 2047:4096] = yt[129:2178]
    return out


_NC = None


def kernel(**inputs):
    global _NC
    if _NC is None:
        _NC = build_nc()
    maps = make_in_maps(inputs)
    res = run_bass_kernel_spmd(_NC, maps, core_ids=list(range(8)))
    return assemble(res.results)
```
